# Optimizing a Trainium2 kernel written in Bass

```python
import math
import jax, jax.numpy as jnp
from jax import lax
import numpy as np

D_MODEL = 1024
BATCH = 32
SEQ = 2048
DEPTH = 4

N_MIXERS = 2
N_META = 16
N_HEADS = 8
HEAD_DIM = D_MODEL // (2 * N_HEADS)
V_DIM = 2 * HEAD_DIM
CONV_WIDTH = 3
CONV_GROUPS = 8
D_FF = 4 * D_MODEL
Q_BLOCK = 128
RMS_EPS = 1e-6
N_ATTN_LAYERS = (DEPTH + 1) // 2
N_CONV_LAYERS = DEPTH // 2

kernel_name = 'hybrid_diffattn_shortconv_sqrelu_trunk'


def _rmsnorm(x, g):
    xf = x.astype(jnp.float32)
    y = xf * lax.rsqrt(jnp.mean(xf * xf, axis=-1, keepdims=True) + RMS_EPS)
    return (y * g.astype(jnp.float32)).astype(x.dtype)


def _alibi_slopes(n_heads):
    return jnp.exp2(-8.0 * jnp.arange(1, n_heads + 1, dtype=jnp.float32) / n_heads)


def _lambda_init(layer_idx):
    return 0.8 - 0.6 * math.exp(-0.3 * layer_idx)


def _query_blocks(total_len):
    blocks = [(0, N_META)]
    start = N_META
    while start < total_len:
        end = min(start + Q_BLOCK, total_len)
        blocks.append((start, end))
        start = end
    return blocks


def _diff_attention(h, w_in, w_out, lam_vec, subln_g, lam_init):
    bsz, L, _ = h.shape
    qkv = jnp.einsum('bld,de->ble', h, w_in)
    q, k, v = jnp.split(qkv, 3, axis=-1)
    q = q.reshape(bsz, L, N_HEADS, 2, HEAD_DIM)
    k = k.reshape(bsz, L, N_HEADS, 2, HEAD_DIM)
    v = v.reshape(bsz, L, N_HEADS, V_DIM)
    lv = lam_vec.astype(jnp.float32)
    lam = jnp.exp(jnp.sum(lv[0] * lv[1])) - jnp.exp(jnp.sum(lv[2] * lv[3])) + lam_init
    slopes = _alibi_slopes(N_HEADS)
    scale = HEAD_DIM ** -0.5
    pos = jnp.arange(L, dtype=jnp.int32)
    outs = []
    for qs, qe in _query_blocks(L):
        s = jnp.einsum('bqhcd,bkhcd->bhcqk', q[:, qs:qe], k[:, :qe]).astype(jnp.float32) * scale
        dist = (pos[qs:qe, None] - pos[None, :qe]).astype(jnp.float32)
        bias = -slopes[:, None, None, None] * dist
        s = jnp.where(dist >= 0, s + bias, -jnp.inf)
        p = jax.nn.softmax(s, axis=-1)
        a = (p[:, :, 0] - lam * p[:, :, 1]).astype(v.dtype)
        outs.append(jnp.einsum('bhqk,bkhe->bqhe', a, v[:, :qe]))
    o = jnp.concatenate(outs, axis=1)
    o = _rmsnorm(o, subln_g) * (1.0 - lam_init)
    return jnp.einsum('ble,ed->bld', o.reshape(bsz, L, N_HEADS * V_DIM), w_out)


def _short_conv(h, w_in, conv_w, w_out):
    bcu = jnp.einsum('bld,de->ble', h, w_in)
    b_gate, c_gate, u = jnp.split(bcu, 3, axis=-1)
    u = c_gate * u
    y = lax.conv_general_dilated(
        u, conv_w[:, None, :].astype(u.dtype), window_strides=(1,),
        padding=[(CONV_WIDTH - 1, 0)], dimension_numbers=('NWC', 'WIO', 'NWC'),
        feature_group_count=D_MODEL)
    return jnp.einsum('bld,de->ble', b_gate * y, w_out)


def _sqrelu_mlp(h, w_up, w_down):
    a = jax.nn.relu(jnp.einsum('bld,df->blf', h, w_up))
    return jnp.einsum('blf,fd->bld', a * a, w_down)


def setup_inputs(seed: int = 0) -> dict:
    key = jax.random.key(seed)
    ks = jax.random.split(key, 10)
    x = jax.random.normal(ks[0], (BATCH, SEQ, D_MODEL), jnp.float32)
    meta_tokens = jax.random.normal(ks[1], (N_META, D_MODEL), jnp.float32)
    norm_g = 1.0 + 0.05 * jax.random.normal(ks[2], (DEPTH, 4, D_MODEL), jnp.float32)
    w_in = jax.random.normal(ks[3], (DEPTH, D_MODEL, 3 * D_MODEL), jnp.float32) * D_MODEL ** -0.5
    w_out = jax.random.normal(ks[4], (DEPTH, D_MODEL, D_MODEL), jnp.float32) * D_MODEL ** -0.5
    lambda_params = 0.1 * jax.random.normal(ks[5], (N_ATTN_LAYERS, 4, HEAD_DIM), jnp.float32)
    subln_g = 1.0 + 0.05 * jax.random.normal(ks[6], (N_ATTN_LAYERS, V_DIM), jnp.float32)
    conv_w = jax.random.normal(ks[7], (N_CONV_LAYERS, CONV_WIDTH, D_MODEL), jnp.float32) * CONV_WIDTH ** -0.5
    w_up = jax.random.normal(ks[8], (DEPTH, D_MODEL, D_FF), jnp.float32) * D_MODEL ** -0.5
    w_down = jax.random.normal(ks[9], (DEPTH, D_FF, D_MODEL), jnp.float32) * D_FF ** -0.5
    return {'x': x, 'meta_tokens': meta_tokens, 'norm_g': norm_g, 'w_in': w_in, 'w_out': w_out,
            'lambda_params': lambda_params, 'subln_g': subln_g, 'conv_w': conv_w,
            'w_up': w_up, 'w_down': w_down}


def reference(x, meta_tokens, norm_g, w_in, w_out, lambda_params, subln_g, conv_w, w_up, w_down):
    bsz = x.shape[0]
    meta = jnp.broadcast_to(meta_tokens[None].astype(x.dtype), (bsz, N_META, D_MODEL))
    hres = jnp.concatenate([meta, x], axis=1)
    for i in range(DEPTH):
        h = _rmsnorm(hres, norm_g[i, 0])
        if i % N_MIXERS == 0:
            m = _diff_attention(h, w_in[i], w_out[i], lambda_params[i // N_MIXERS],
                                subln_g[i // N_MIXERS], _lambda_init(i))
        else:
            m = _short_conv(h, w_in[i], conv_w[i // N_MIXERS], w_out[i])
        hres = hres + _rmsnorm(m, norm_g[i, 1])
        h = _rmsnorm(hres, norm_g[i, 2])
        hres = hres + _rmsnorm(_sqrelu_mlp(h, w_up[i], w_down[i]), norm_g[i, 3])
    return hres[:, N_META:]
```

```python
import math
import os
import numpy as np
from contextlib import ExitStack
import concourse.bass as bass
import concourse.mybir as mybir
from concourse.bass_utils import run_bass_kernel_spmd

F32 = mybir.dt.float32
BF16 = mybir.dt.bfloat16
AF = mybir.ActivationFunctionType
ALU = mybir.AluOpType

D = 1024
NC8 = 8
SEQ = 2048
NMETA = 16
L = SEQ + NMETA
DEPTH = 4
NH = 8
DFF = 4096
EPS = 1e-6
TT = [(0, 16)] + [(16 + 512 * i, 16 + 512 * (i + 1)) for i in range(4)]
KT = [(0, 16)] + [(16 + 128 * j, 16 + 128 * (j + 1)) for j in range(16)]
ENGS = ("pe", "act", "dve", "pool", "sp")
NDMASEM = 24


def lam_init(i):
    return 0.8 - 0.6 * math.exp(-0.3 * i)


class Op:
    __slots__ = ("id", "eng", "fn", "deps", "dma", "marked", "count", "sem", "prev_dma")

    def __init__(self, id, eng, fn, deps, dma):
        self.id, self.eng, self.fn, self.deps, self.dma = id, eng, fn, deps, dma
        self.marked = False
        self.count = 0
        self.sem = None
        self.prev_dma = None


class Plan:
    def __init__(self):
        self.ops = []
        self.q = {e: [] for e in ENGS}
        self.lastw = {}
        self.readers = {}
        self.bar = set()
        self.since_bar = []
        self.dma_rr = 0
        self.dma_last = [None] * NDMASEM

    def op(self, eng, fn, reads=(), writes=(), dma=False, nobar=False, extra=(), untracked=False):
        deps = set(extra)
        for k in reads:
            w = self.lastw.get(k)
            if w is not None:
                deps.add(w)
        for k in writes:
            w = self.lastw.get(k)
            if w is not None:
                deps.add(w)
            for r in self.readers.get(k, ()):
                deps.add(r)
        if not nobar:
            deps |= self.bar
        o = Op(len(self.ops), eng, fn, deps, dma)
        if dma:
            o.sem = self.dma_rr
            o.prev_dma = self.dma_last[o.sem]
            self.dma_last[o.sem] = o.id
            self.dma_rr = (self.dma_rr + 1) % NDMASEM
        self.ops.append(o)
        self.q[eng].append(o)
        for k in reads:
            self.readers.setdefault(k, []).append(o.id)
        for k in writes:
            self.lastw[k] = o.id
            self.readers[k] = []
        if not untracked:
            self.since_bar.append(o.id)
        return o.id

    def barrier(self):
        last = {}
        for oid in self.since_bar:
            o = self.ops[oid]
            if o.dma:
                last[("dma", oid)] = oid
            else:
                last[o.eng] = oid
        self.bar = set(last.values())
        self.since_bar = list(self.bar)

    def emit(self, nc, es, final_ops):
        ops = self.ops
        for o in ops:
            for d in o.deps:
                ops[d].marked = True
            if o.prev_dma is not None:
                ops[o.prev_dma].marked = True
        for d in final_ops:
            ops[d].marked = True
        esem = {e: es.enter_context(nc.semaphore("sem_" + e)) for e in ("pe", "act", "dve", "pool")}
        dsem = [es.enter_context(nc.semaphore("sem_dma%d" % i)) for i in range(NDMASEM)]
        cnt = {e: 0 for e in ENGS}
        dcnt = [0] * NDMASEM
        for o in ops:
            if o.dma:
                dcnt[o.sem] += 16
                o.count = dcnt[o.sem]
            elif o.marked:
                cnt[o.eng] += 1
                o.count = cnt[o.eng]
        block = es.enter_context(nc.Block())

        def run(engname, e):
            waited = {}

            def wait_for(d):
                od = ops[d]
                if od.dma:
                    key, sem, val = ("d", od.sem), dsem[od.sem], od.count
                else:
                    if od.eng == "pe" and engname == "pe":
                        return
                    key, sem, val = ("e", od.eng), esem[od.eng], od.count
                if waited.get(key, 0) >= val:
                    return
                waited[key] = val
                e.wait_ge(sem, val)

            for o in self.q[engname]:
                for d in sorted(o.deps):
                    wait_for(d)
                if o.prev_dma is not None:
                    wait_for(o.prev_dma)
                ins = o.fn(e)
                if o.dma:
                    ins.then_inc(dsem[o.sem], 16)
                elif o.marked:
                    ins.then_inc(esem[o.eng], 1)
            if engname == "sp":
                for d in final_ops:
                    wait_for(d)

        @block.tensor
        def _(e):
            run("pe", e)

        @block.scalar
        def _(e):
            run("act", e)

        @block.vector
        def _(e):
            run("dve", e)

        @block.gpsimd
        def _(e):
            run("pool", e)

        @block.sync
        def _(e):
            run("sp", e)


ARENA_WORDS = 53100
_T = lambda k, d: int(os.environ.get("KT_" + k, d))
T_PARTB = _T("PARTB", 9)
T_OPD = _T("OPD", 2)
T_OPN = _T("OPN", 28)
T_MD1 = _T("MD1", 1)
T_MD2 = _T("MD2", 1)
T_MN = _T("MN", 24)
T_POOLMASK = _T("POOLMASK", 0)


def build(n_seq=4, layers=(0, 1, 2, 3), debug_out=None):
    nc = bass.Bass("TRN2", target_bir_lowering=False)
    xT = nc.dram_tensor("xT", [n_seq, D, SEQ], F32, kind="ExternalInput").ap()
    metaT = nc.dram_tensor("metaT", [D, NMETA], F32, kind="ExternalInput").ap()
    cst_d = nc.dram_tensor("cst", [128, 192], F32, kind="ExternalInput").ap()
    lam_d = nc.dram_tensor("lamp", [128, 512], F32, kind="ExternalInput").ap()
    tri_d = nc.dram_tensor("tri", [128, 128], F32, kind="ExternalInput").ap()
    qaug_d = nc.dram_tensor("qaug", [NH, 4, L], F32, kind="ExternalInput").ap()
    kaug_d = nc.dram_tensor("kaug", [NH, 4, L], F32, kind="ExternalInput").ap()
    w_in_d = nc.dram_tensor("w_in_g", [DEPTH, NC8, D, 384], F32, kind="ExternalInput").ap()
    w_out_d = nc.dram_tensor("w_out", [DEPTH, D, D], F32, kind="ExternalInput").ap()
    w_up_d = nc.dram_tensor("w_up", [DEPTH, D, DFF], F32, kind="ExternalInput").ap()
    w_dn_d = nc.dram_tensor("w_down", [DEPTH, DFF, D], F32, kind="ExternalInput").ap()
    yT = nc.dram_tensor("yT", [n_seq, D, SEQ if debug_out is None else L], F32, kind="ExternalOutput").ap()

    es = ExitStack()
    arena = es.enter_context(nc.sbuf_tensor("arena", [128, ARENA_WORDS], F32))
    PS = [es.enter_context(nc.psum_tensor("ps%d" % i, [128, 512], F32)) for i in range(8)]
    off = [0]

    def carve(words):
        a = off[0]
        off[0] += words
        assert off[0] <= ARENA_WORDS, off[0]
        return arena[:, a:a + words]

    def bf(ap):
        return ap.bitcast(BF16)

    HRES = carve(NC8 * L).rearrange("p (c t) -> p c t", c=NC8)
    H = bf(carve(NC8 * L // 2)).rearrange("p (c t) -> p c t", c=NC8)
    OREG = carve(8320)
    O = bf(OREG[:, 0:NC8 * L // 2]).rearrange("p (c t) -> p c t", c=NC8)
    MACC = OREG.rearrange("p (c t) -> p c t", c=NC8)
    WB = [bf(carve(2048)) for _ in range(4)]
    CST = carve(192)
    ONES = bf(carve(64))
    TRI = bf(carve(64))
    LAMS = carve(16)
    SGS = carve(2)
    EPSB = carve(2)
    SQP = bf(carve(NC8 * 256)).rearrange("p (c t) -> p c t", c=NC8)
    RBP = carve(512)
    RB2 = carve(512)
    scratch0 = off[0]

    G = CST[:, 0:128].rearrange("p (a c) -> p a c", c=NC8)
    CW = CST[:, 128:176].rearrange("p (a k c) -> p a k c", a=2, k=3)
    SG = CST[:, 176:178]

    P = Plan()
    wb_rr = [0]

    def wslot():
        s = wb_rr[0]
        wb_rr[0] = (s + 1) % 4
        return s

    ps_rr = [0]

    def psbank(lo=0, hi=8):
        b = lo + ps_rr[0] % (hi - lo)
        ps_rr[0] += 1
        return b

    P.op("sp", lambda e: e.dma_start(out=CST, in_=cst_d), writes=["cst"], dma=True)
    LAMP_raw = arena[:, ARENA_WORDS - 512:ARENA_WORDS]
    LAMP = LAMP_raw.rearrange("p (a k d) -> p a k d", a=2, k=4)
    P.op("sp", lambda e: e.dma_start(out=LAMP_raw, in_=lam_d), writes=["lamp"], dma=True)
    P.op("pool", lambda e: e.dma_start(out=TRI, in_=tri_d), writes=["tri"], dma=True)
    P.op("dve", lambda e: e.memset(ONES, 1.0), writes=["ones"])
    P.op("dve", lambda e: e.memset(EPSB, EPS), writes=["epsb"])

    def lam_setup(a, li):
        t = scratch_view(0, 256)
        P.op("dve", lambda e: e.tensor_tensor(out=t[:, 0:64], in0=LAMP[:, a, 0, :], in1=LAMP[:, a, 1, :], op=ALU.mult),
             reads=["lamp"], writes=["lamt0"])
        P.op("dve", lambda e: e.tensor_tensor(out=t[:, 64:128], in0=LAMP[:, a, 2, :], in1=LAMP[:, a, 3, :], op=ALU.mult),
             reads=["lamp"], writes=["lamt1"])
        P.op("dve", lambda e: e.reduce_sum(out=t[:, 128:129], in_=t[:, 0:64], axis=mybir.AxisListType.X),
             reads=["lamt0"], writes=["lamt2"])
        P.op("dve", lambda e: e.reduce_sum(out=t[:, 129:130], in_=t[:, 64:128], axis=mybir.AxisListType.X),
             reads=["lamt1"], writes=["lamt3"])
        P.op("act", lambda e: e.activation(out=t[:, 130:132], in_=t[:, 128:130], func=AF.Exp),
             reads=["lamt2", "lamt3"], writes=["lamt4"])
        P.op("dve", lambda e: e.scalar_tensor_tensor(out=LAMS[:, 4 * a:4 * a + 1], in0=t[:, 131:132],
                                                     scalar=-lam_init(li), in1=t[:, 130:131],
                                                     op0=ALU.add, op1=ALU.subtract),
             reads=["lamt4"], writes=["lams%d" % a])
        P.op("dve", lambda e: e.tensor_scalar(out=SGS[:, a:a + 1], in0=SG[:, a:a + 1], scalar1=1.0 - lam_init(li),
                                              scalar2=None, op0=ALU.mult),
             reads=["cst"], writes=["sgs%d" % a])

    def scratch_view(o, words):
        assert scratch0 + o + words <= ARENA_WORDS, (o, words)
        return arena[:, scratch0 + o:scratch0 + o + words]

    for a, li in ((0, 0), (1, 2)):
        lam_setup(a, li)
    P.barrier()

    def load_w(slot, dram_ap, shape3):
        kc, cols = shape3
        dst = WB[slot][:, 0:kc * cols].rearrange("p (k c) -> p k c", k=kc)
        P.op("pool", lambda e: e.dma_start(out=dst, in_=dram_ap.rearrange("(k p) c -> p k c", p=128)),
             writes=[("wb", slot)], dma=True, nobar=True)
        return dst

    bg = []
    bgc = [0, 0]

    def push(steps):
        bg.extend(steps)
        bgc[0] += len(steps)
        return bgc[0]

    def drain(n=1):
        for _ in range(n):
            if bg:
                bg.pop(0)()
                bgc[1] += 1

    def flush_to(target):
        while bg and bgc[1] < target:
            bg.pop(0)()
            bgc[1] += 1

    def flush():
        flush_to(bgc[0])

    def run_steps(steps):
        for st in steps:
            st()

    def norm_stats_steps(src3, a, b, SQ, RB, inv_n, keys_r, banks=(0, 8)):
        T = b - a
        cell = {}

        def s1():
            P.op("act", lambda e: e.activation(out=SQ[:, :, 0:T], in_=src3, func=AF.Square),
                 reads=keys_r, writes=["sq"] + sqk)

        def s2():
            bank = psbank(*banks)

            def mm(e):
                for c in range(NC8):
                    i = e.matmul(PS[bank][:, 0:T], lhsT=ONES, rhs=SQ[:, c, 0:T], start=(c == 0), stop=(c == NC8 - 1))
                return i
            P.op("pe", mm, reads=["sq", "ones"], writes=[("ps", bank)])
            P.op("act", lambda e: e.activation(out=RB[:, 0:T], in_=PS[bank][:, 0:T], func=AF.Ln, scale=inv_n, bias=EPSB[:, 0:1]),
                 reads=[("ps", bank), "epsb"], writes=["rb"])
            P.op("act", lambda e: e.activation(out=RB[:, 0:T], in_=RB[:, 0:T], func=AF.Exp, scale=-0.5),
                 reads=["rb"], writes=["rb"])
        nop = lambda: None
        return [s1, nop, nop, nop, nop, nop, nop, s2, nop, nop, nop]

    def pre_norm_steps(li, gi, ti, SQ, RB, banks=(0, 8)):
        a, b = TT[ti]
        T = b - a
        steps = norm_stats_steps(HRES[:, :, a:b], a, b, SQ, RB, 1.0 / D, [("hres", ti)], banks)

        def mk(c):
            def st():
                P.op("dve", lambda e: e.scalar_tensor_tensor(
                    out=H[:, c, a:b], in0=HRES[:, c, a:b], scalar=G[:, 4 * li + gi, c:c + 1], in1=RB[:, 0:T],
                    op0=ALU.mult, op1=ALU.mult),
                    reads=[("hres", ti), "rb", "cst"], writes=[("h", ti)])
            return st
        return steps + [mk(c) for c in range(NC8)]

    def pre_norm(li, gi, ti, SQ, RB, banks=(0, 8)):
        run_steps(pre_norm_steps(li, gi, ti, SQ, RB, banks))

    def post_norm_steps(li, gi, ti, M3, mkeys, SQ, RB, banks=(0, 8), xkeys=()):
        a, b = TT[ti]
        T = b - a
        steps = norm_stats_steps(M3, a, b, SQ, RB, 1.0 / D, list(mkeys) + list(xkeys), banks)

        def mk(c):
            def st():
                P.op("pool", lambda e: e.tensor_tensor(out=M3[:, c, :], in0=M3[:, c, :], in1=RB[:, 0:T], op=ALU.mult),
                     reads=[mkeys[c], "rb"] + list(xkeys), writes=[mkeys[c]])
                P.op("dve", lambda e: e.scalar_tensor_tensor(
                    out=HRES[:, c, a:b], in0=M3[:, c, :], scalar=G[:, 4 * li + gi, c:c + 1], in1=HRES[:, c, a:b],
                    op0=ALU.mult, op1=ALU.add),
                    reads=[mkeys[c], "cst", ("hres", ti)] + list(xkeys), writes=[("hres", ti)])
            return st
        return steps + [mk(c) for c in range(NC8)]

    sqk = [("sq", c) for c in range(NC8)]

    def post_head(T, SQ, RB, banks=(0, 8)):
        bank = psbank(*banks)

        def mm(e):
            for c in range(NC8):
                i = e.matmul(PS[bank][:, 0:T], lhsT=ONES, rhs=SQ[:, c, 0:T], start=(c == 0), stop=(c == NC8 - 1))
            return i
        P.op("pe", mm, reads=sqk + ["sq", "ones"], writes=[("ps", bank)])
        P.op("act", lambda e: e.activation(out=RB[:, 0:T], in_=PS[bank][:, 0:T], func=AF.Ln, scale=1.0 / D, bias=EPSB[:, 0:1]),
             reads=[("ps", bank), "epsb"], writes=["rb"])
        P.op("act", lambda e: e.activation(out=RB[:, 0:T], in_=RB[:, 0:T], func=AF.Exp, scale=-0.5),
             reads=["rb"], writes=["rb"])

    def post_tail_steps(li, gi, ti, M3, mkeys, RB, nxt, banks=(0, 8), xkeys=(), nops=T_MN):
        a, b = TT[ti]
        T = b - a
        xk = list(xkeys)

        def s0():
            for c in [c for c in range(NC8) if (T_POOLMASK >> c) & 1]:
                P.op("pool", lambda e, c=c: e.tensor_tensor(out=M3[:, c, :], in0=M3[:, c, :], in1=RB[:, 0:T], op=ALU.mult),
                     reads=[mkeys[c], "rb"] + xk, writes=[mkeys[c]])

        def mk(c):
            def st():
                if not (T_POOLMASK >> c) & 1:
                    P.op("dve", lambda e: e.tensor_tensor(out=M3[:, c, :], in0=M3[:, c, :], in1=RB[:, 0:T], op=ALU.mult),
                         reads=[mkeys[c], "rb"] + xk, writes=[mkeys[c]])
                P.op("dve", lambda e: e.scalar_tensor_tensor(
                    out=HRES[:, c, a:b], in0=M3[:, c, :], scalar=G[:, 4 * li + gi, c:c + 1], in1=HRES[:, c, a:b],
                    op0=ALU.mult, op1=ALU.add),
                    reads=[mkeys[c], "cst", ("hres", ti)] + xk, writes=[("hres", ti)])
                if nxt is not None:
                    P.op("act", lambda e: e.activation(out=H[:, c, a:b], in_=HRES[:, c, a:b], func=AF.Square),
                         reads=[("hres", ti)], writes=[("h", ti)])
            return st
        steps = [s0] + [mk(c) for c in (0, 2, 4, 6, 1, 3, 5, 7)]
        if nxt is None:
            return steps
        li2, gi2 = nxt

        def s2():
            bank = psbank(*banks)

            def mm(e):
                for c in range(NC8):
                    i = e.matmul(PS[bank][:, 0:T], lhsT=ONES, rhs=H[:, c, a:b], start=(c == 0), stop=(c == NC8 - 1))
                return i
            P.op("pe", mm, reads=[("h", ti), "ones"], writes=[("ps", bank)])
            P.op("act", lambda e: e.activation(out=RB2[:, 0:T], in_=PS[bank][:, 0:T], func=AF.Ln, scale=1.0 / D, bias=EPSB[:, 0:1]),
                 reads=[("ps", bank), "epsb"], writes=["rb2"])
            P.op("act", lambda e: e.activation(out=RB2[:, 0:T], in_=RB2[:, 0:T], func=AF.Exp, scale=-0.5),
                 reads=["rb2"], writes=["rb2"])

        def mk2(c):
            def st():
                P.op("dve", lambda e: e.scalar_tensor_tensor(
                    out=H[:, c, a:b], in0=HRES[:, c, a:b], scalar=G[:, 4 * li2 + gi2, c:c + 1], in1=RB2[:, 0:T],
                    op0=ALU.mult, op1=ALU.mult),
                    reads=[("hres", ti), "rb2", "cst"], writes=[("h", ti)])
            return st
        nop = lambda: None
        return steps + [nop] * nops + [s2, nop, nop] + [mk2(c) for c in range(NC8)]

    def phase_attention(li, do_pre):
        a_idx = li // 2
        so = [0]

        def sv(words):
            v = scratch_view(so[0], words)
            so[0] += words
            return v
        QA, QB, KA, KB = [bf(sv(L // 2)) for _ in range(4)]
        VH = bf(sv(17 * 64)).rearrange("p (j e) -> p j e", j=17)
        PT = [bf(sv(256)) for _ in range(3)]
        R0s, R1s = [sv(512), sv(512)], [sv(512), sv(512)]
        SQ1s = [bf(sv(256))] * 2
        fin_rr = [0]
        pendB = []
        SQ, RB = SQP, RBP
        if do_pre:
            for ti in range(5):
                pre_norm(li, 0, ti, SQ, RB)
        pt_rr = [0]
        for h in range(NH):
            slot = wslot()
            W = load_w(slot, w_in_d[li, h], (NC8, 384))
            for dst, src, key in ((QA, qaug_d, "qaug"), (QB, qaug_d, "qaug2"), (KA, kaug_d, "kaug"), (KB, kaug_d, "kaug2")):
                P.op("pool", lambda e, dst=dst, src=src, h=h: e.dma_start(out=dst[64:68, :], in_=src[h]),
                     writes=[key], dma=True)
            for ti, (a, b) in enumerate(TT):
                T = b - a
                if ti == 4:
                    flush()
                for which, (XA, XB) in enumerate(((QA, QB), (KA, KB))):
                    drain(5)
                    bank = 6 + (ps_rr[0] % 2)
                    ps_rr[0] += 1

                    def mm(e, which=which, bank=bank, a=a, b=b, T=T, W=W):
                        for kc in range(NC8):
                            i = e.matmul(PS[bank][:, 0:T], lhsT=W[:, kc, which * 128:(which + 1) * 128],
                                         rhs=H[:, kc, a:b], start=(kc == 0), stop=(kc == NC8 - 1))
                        return i
                    P.op("pe", mm, reads=[("wb", slot), ("h", ti)], writes=[("ps", bank)])
                    nm = "qk"[which]
                    P.op("act", lambda e, XA=XA, bank=bank, a=a, b=b, T=T: e.activation(
                        out=XA[0:64, a:b], in_=PS[bank][0:64, 0:T], func=AF.Copy),
                        reads=[("ps", bank)], writes=[(nm + "a", ti)])
                    P.op("act", lambda e, XB=XB, bank=bank, a=a, b=b, T=T: e.activation(
                        out=XB[0:64, a:b], in_=PS[bank][64:128, 0:T], func=AF.Copy),
                        reads=[("ps", bank)], writes=[(nm + "b", ti)])
            groups = [[0]] + [[1 + 4 * g + m for m in range(4)] for g in range(4)]
            for gi, grp in enumerate(groups):
                bank = 6 + (ps_rr[0] % 2)
                ps_rr[0] += 1

                def mm(e, grp=grp, bank=bank, W=W):
                    for m, j in enumerate(grp):
                        ka, kb = KT[j]
                        for kc in range(NC8):
                            i = e.matmul(PS[bank][0:kb - ka, 128 * m:128 * (m + 1)], lhsT=H[:, kc, ka:kb],
                                         rhs=W[:, kc, 256:384], start=(kc == 0), stop=(kc == NC8 - 1))
                    return i
                P.op("pe", mm, reads=[("wb", slot), ("h", gi)], writes=[("ps", bank)])
                if gi == 0:
                    P.op("dve", lambda e, bank=bank: e.tensor_copy(out=VH[0:16, 0, :], in_=PS[bank][0:16, 0:128]),
                         reads=[("ps", bank)], writes=[("v", gi)])
                else:
                    j0 = grp[0]
                    P.op("dve", lambda e, bank=bank, j0=j0: e.tensor_copy(
                        out=VH[:, j0:j0 + 4, :], in_=PS[bank][:, :].rearrange("p (j e) -> p j e", j=4)),
                        reads=[("ps", bank)], writes=[("v", gi)])
            def attend_tile(h, qi):
                qa, qb = TT[qi]
                Tq = qb - qa
                jmax = 0 if qi == 0 else 4 * qi
                units = [(j, c) for j in range(jmax + 1) for c in range(2)]
                pend = []

                def issue_score(j, c):
                    ka, kb = KT[j]
                    kk = kb - ka
                    diag = (qi == 0) or (j > 4 * (qi - 1))
                    c0 = 0 if (qi == 0 or not diag) else 128 * (j - 4 * (qi - 1) - 1)
                    sbank = ps_rr[0] % 2
                    ps_rr[0] += 1
                    pslot = pt_rr[0] % 3
                    pt_rr[0] += 1
                    Kt, Qt = (KA, QA) if c == 0 else (KB, QB)
                    kti = 0 if j == 0 else 1 + (j - 1) // 4
                    sfx = "a" if c == 0 else "b"
                    P.op("pe", lambda e: e.matmul(PS[sbank][0:kk, c0:Tq], lhsT=Kt[0:68, ka:kb], rhs=Qt[0:68, qa + c0:qb],
                                                  start=True, stop=True),
                         reads=[("k" + sfx, kti), ("q" + sfx, qi), "kaug" + ("" if c == 0 else "2"),
                                "qaug" + ("" if c == 0 else "2")],
                         writes=[("ps", sbank)])
                    P.op("act", lambda e: e.activation(out=PT[pslot][0:kk, c0:Tq], in_=PS[sbank][0:kk, c0:Tq],
                                                       func=AF.Exp, scale=0.125),
                         reads=[("ps", sbank)], writes=[("pt", pslot)])
                    if diag:
                        w = min(128, Tq - c0)
                        P.op("dve", lambda e: e.tensor_tensor(out=PT[pslot][0:kk, c0:c0 + w], in0=PT[pslot][0:kk, c0:c0 + w],
                                                              in1=TRI[0:kk, 0:w], op=ALU.mult),
                             reads=[("pt", pslot), "tri"], writes=[("pt", pslot)])
                    return (j, c, kk, c0, pslot)

                def issue_pv(u):
                    j, c, kk, c0, pslot = u
                    first, last = (j == 0), (j == jmax)
                    kti = 0 if j == 0 else 1 + (j - 1) // 4

                    def mm(e):
                        e.matmul(PS[2 + c][:, c0:Tq], lhsT=VH[0:kk, j, :], rhs=PT[pslot][0:kk, c0:Tq], start=first, stop=last)
                        return e.matmul(PS[4 + c][:, c0:Tq], lhsT=ONES[0:kk, :], rhs=PT[pslot][0:kk, c0:Tq], start=first, stop=last)
                    P.op("pe", mm, reads=[("pt", pslot), ("v", kti), "ones"], writes=[("ps", 2 + c), ("ps", 4 + c)])

                for n_u, (j, c) in enumerate(units):
                    pend.append(issue_score(j, c))
                    if len(pend) > 2:
                        issue_pv(pend.pop(0))
                    if n_u == T_PARTB and pendB:
                        pendB.pop(0)()
                while pend:
                    issue_pv(pend.pop(0))
                fb = fin_rr[0] % 2
                fin_rr[0] += 1
                R0, R1, SQ1 = R0s[fb], R1s[fb], SQ1s[fb]
                kr0, kr1, ksq = ("r0", fb), ("r1", fb), ("sq1", 0)
                for Rx, kx, sb in ((R0, kr0, 4), (R1, kr1, 5)):
                    P.op("act", lambda e, Rx=Rx, sb=sb: e.activation(out=Rx[:, 0:Tq], in_=PS[sb][:, 0:Tq], func=AF.Ln),
                         reads=[("ps", sb)], writes=[kx])
                    P.op("act", lambda e, Rx=Rx: e.activation(out=Rx[:, 0:Tq], in_=Rx[:, 0:Tq], func=AF.Exp, scale=-1.0),
                         reads=[kx], writes=[kx])
                P.op("dve", lambda e: e.tensor_tensor(out=R0[:, 0:Tq], in0=PS[2][:, 0:Tq], in1=R0[:, 0:Tq], op=ALU.mult),
                     reads=[("ps", 2), kr0], writes=[kr0])
                P.op("dve", lambda e: e.tensor_tensor(out=R1[:, 0:Tq], in0=PS[3][:, 0:Tq], in1=R1[:, 0:Tq], op=ALU.mult),
                     reads=[("ps", 3), kr1], writes=[kr1])

                def partB():
                    P.op("dve", lambda e: e.scalar_tensor_tensor(out=R0[:, 0:Tq], in0=R1[:, 0:Tq], scalar=LAMS[:, 4 * a_idx:4 * a_idx + 1],
                                                                 in1=R0[:, 0:Tq], op0=ALU.mult, op1=ALU.add),
                         reads=[kr0, kr1, "lams%d" % a_idx], writes=[kr0])
                    P.op("dve", lambda e: e.tensor_tensor(out=SQ1[:, 0:Tq], in0=R0[:, 0:Tq], in1=R0[:, 0:Tq], op=ALU.mult), reads=[kr0], writes=[ksq])
                    sbank = 6 + ps_rr[0] % 2
                    ps_rr[0] += 1
                    P.op("pe", lambda e: e.matmul(PS[sbank][:, 0:Tq], lhsT=ONES, rhs=SQ1[:, 0:Tq], start=True, stop=True),
                         reads=[ksq, "ones"], writes=[("ps", sbank)])
                    P.op("act", lambda e: e.activation(out=R1[:, 0:Tq], in_=PS[sbank][:, 0:Tq], func=AF.Ln, scale=1.0 / 128, bias=EPSB[:, 0:1]),
                         reads=[("ps", sbank), "epsb"], writes=[kr1])
                    P.op("act", lambda e: e.activation(out=R1[:, 0:Tq], in_=R1[:, 0:Tq], func=AF.Exp, scale=-0.5),
                         reads=[kr1], writes=[kr1])
                    P.op("dve", lambda e: e.scalar_tensor_tensor(out=O[:, h, qa:qb], in0=R0[:, 0:Tq], scalar=SGS[:, a_idx:a_idx + 1],
                                                                 in1=R1[:, 0:Tq], op0=ALU.mult, op1=ALU.mult),
                         reads=[kr0, kr1, "sgs%d" % a_idx], writes=[("o", qi)])
                pendB.append(partB)

            for qi in (1, 2, 3, 4, 0):
                attend_tile(h, qi)
                if qi == 0:
                    while pendB:
                        pendB.pop(0)()
        while pendB:
            pendB.pop(0)()
        P.barrier()

    def phase_conv(li, do_pre):
        ci = li // 2
        so = [0]

        def sv(words):
            v = scratch_view(so[0], words)
            so[0] += words
            return v
        BG = sv(L)
        UP = sv(L + 2)
        Y = sv(L)
        CT = sv(512)
        SQ, RB = SQP, RBP
        P.op("dve", lambda e: e.memset(UP[:, 0:2], 0.0), writes=["up0"])
        if do_pre:
            for ti in range(5):
                pre_norm(li, 0, ti, SQ, RB)
        for j in range(NC8):
            slot = wslot()
            W = load_w(slot, w_in_d[li, j], (NC8, 384))
            for ti, (a, b) in enumerate(TT):
                T = b - a
                banks = []
                if ti == 4:
                    flush()
                for which in range(3):
                    drain(4)
                    bank = psbank()
                    banks.append(bank)

                    def mm(e, which=which, bank=bank, a=a, b=b, T=T, W=W):
                        for kc in range(NC8):
                            i = e.matmul(PS[bank][:, 0:T], lhsT=W[:, kc, which * 128:(which + 1) * 128],
                                         rhs=H[:, kc, a:b], start=(kc == 0), stop=(kc == NC8 - 1))
                        return i
                    P.op("pe", mm, reads=[("wb", slot), ("h", ti)], writes=[("ps", bank)])
                P.op("act", lambda e, bank=banks[0], a=a, b=b, T=T: e.activation(out=BG[:, a:b], in_=PS[bank][:, 0:T], func=AF.Copy),
                     reads=[("ps", banks[0])], writes=[("bg", ti)])
                P.op("act", lambda e, bank=banks[1], T=T: e.activation(out=CT[:, 0:T], in_=PS[bank][:, 0:T], func=AF.Copy),
                     reads=[("ps", banks[1])], writes=["ct"])
                P.op("dve", lambda e, bank=banks[2], a=a, b=b, T=T: e.tensor_tensor(
                    out=UP[:, 2 + a:2 + b], in0=PS[bank][:, 0:T], in1=CT[:, 0:T], op=ALU.mult),
                    reads=[("ps", banks[2]), "ct"], writes=[("up", ti)])
            upk = [("up", t) for t in range(5)] + ["up0"]
            P.op("dve", lambda e, j=j: e.tensor_scalar(out=Y, in0=UP[:, 2:L + 2], scalar1=CW[:, ci, 2, j:j + 1], scalar2=None, op0=ALU.mult),
                 reads=upk + ["cst"], writes=["y"])
            P.op("dve", lambda e, j=j: e.scalar_tensor_tensor(out=Y, in0=UP[:, 1:L + 1], scalar=CW[:, ci, 1, j:j + 1], in1=Y,
                                                              op0=ALU.mult, op1=ALU.add),
                 reads=upk + ["cst", "y"], writes=["y"])
            P.op("dve", lambda e, j=j: e.scalar_tensor_tensor(out=Y, in0=UP[:, 0:L], scalar=CW[:, ci, 0, j:j + 1], in1=Y,
                                                              op0=ALU.mult, op1=ALU.add),
                 reads=upk + ["cst", "y"], writes=["y"])
            P.op("dve", lambda e, j=j: e.tensor_tensor(out=O[:, j, :], in0=BG, in1=Y, op=ALU.mult),
                 reads=["y"] + [("bg", t) for t in range(5)], writes=[("o", t) for t in range(5)])
        P.barrier()

    def phase_outproj(li):
        so = [0]

        def sv(words):
            v = scratch_view(so[0], words)
            so[0] += words
            return v
        Ms = [sv(NC8 * 512).rearrange("p (c t) -> p c t", c=NC8) for _ in range(2)]
        SQ, RB = SQP, RBP
        s0, s1 = wslot(), wslot()
        W0 = load_w(s0, w_out_d[li][:, 0:512], (NC8, 512))
        W1 = load_w(s1, w_out_d[li][:, 512:1024], (NC8, 512))

        order = [1, 2, 3, 4, 0]
        for n, ti in enumerate(order):
            a, b = TT[ti]
            T = b - a
            mb = n % 2
            for dc in range(NC8):
                bank = psbank(0, 6)
                Wx, sl, col = (W0, s0, dc * 128) if dc < 4 else (W1, s1, (dc - 4) * 128)

                def mm(e, Wx=Wx, col=col, bank=bank, a=a, b=b, T=T):
                    for kc in range(NC8):
                        i = e.matmul(PS[bank][:, 0:T], lhsT=Wx[:, kc, col:col + 128], rhs=O[:, kc, a:b],
                                     start=(kc == 0), stop=(kc == NC8 - 1))
                    return i
                P.op("pe", mm, reads=[("wb", sl), ("o", ti)], writes=[("ps", bank)])
                P.op("act", lambda e, dc=dc, bank=bank, T=T, mb=mb: e.activation(out=Ms[mb][:, dc, 0:T], in_=PS[bank][:, 0:T], func=AF.Copy),
                     reads=[("ps", bank)], writes=[("m", mb, dc)])
                P.op("act", lambda e, dc=dc, bank=bank, T=T: e.activation(out=SQ[:, dc, 0:T], in_=PS[bank][:, 0:T], func=AF.Square),
                     reads=[("ps", bank)], writes=[("sq", dc)])
                drain(T_OPD)
            flush()
            post_head(T, SQ, RB, banks=(6, 8))
            push(post_tail_steps(li, 1, ti, Ms[mb][:, :, 0:T], [("m", mb, dc) for dc in range(NC8)], RB, (li, 2), banks=(6, 8), nops=T_OPN))
        flush()
        P.barrier()

    def phase_mlp(li, next_li):
        so = [0]

        def sv(words):
            v = scratch_view(so[0], words)
            so[0] += words
            return v
        U = [bf(sv(1024)).rearrange("p (f t) -> p f t", f=4) for _ in range(2)]
        RL = [sv(512) for _ in range(2)]
        SQ, RB = SQP, RBP
        MC = sv(NC8 * 512).rearrange("p (c t) -> p c t", c=NC8)
        halves = [[0, 1, 2], [3, 4]]
        mview = {0: ("Z", MACC[:, :, 0:16]), 1: ("A", MACC[:, :, 16:528]), 2: ("B", MACC[:, :, 528:1040]),
                 3: ("C", MC[:, :, :]), 4: ("A", MACC[:, :, 16:528])}
        u_rr = [0]
        rl_rr = [0]
        tailA = [0]

        def tail_steps(ti):
            sl, mv = mview[ti]
            xk = [("o", t) for t in range(5)] if sl != "C" else []
            return post_tail_steps(li, 3, ti, mv, [("macc", sl, dc) for dc in range(NC8)], RB,
                                   (next_li, 0) if next_li is not None else None, xkeys=xk)

        for hi, tiles in enumerate(halves):
            for g in range(8):
                su, sd = wslot(), wslot()
                WU = load_w(su, w_up_d[li][:, g * 512:(g + 1) * 512], (NC8, 512))
                WD = load_w(sd, w_dn_d[li][g * 512:(g + 1) * 512, :], (4, 1024))
                for n, ti in enumerate(tiles):
                    a, b = TT[ti]
                    T = b - a
                    sl, mv = mview[ti]
                    ub = u_rr[0] % 2
                    u_rr[0] += 1
                    if hi == 1 and g == 0 and ti == 4:
                        flush_to(tailA[0])
                    for f in range(4):
                        bank = psbank()

                        def mm(e, f=f, bank=bank, a=a, b=b, T=T, WU=WU):
                            for kc in range(NC8):
                                i = e.matmul(PS[bank][:, 0:T], lhsT=WU[:, kc, f * 128:(f + 1) * 128], rhs=H[:, kc, a:b],
                                             start=(kc == 0), stop=(kc == NC8 - 1))
                            return i
                        P.op("pe", mm, reads=[("wb", su), ("h", ti)], writes=[("ps", bank)])
                        rb_ = rl_rr[0] % 2
                        rl_rr[0] += 1
                        P.op("act", lambda e, bank=bank, rb_=rb_, T=T: e.activation(out=RL[rb_][:, 0:T], in_=PS[bank][:, 0:T], func=AF.Relu),
                             reads=[("ps", bank)], writes=[("rl", rb_)])
                        P.op("dve", lambda e, f=f, ub=ub, rb_=rb_, T=T: e.tensor_tensor(
                            out=U[ub][:, f, 0:T], in0=RL[rb_][:, 0:T], in1=RL[rb_][:, 0:T], op=ALU.mult),
                            reads=[("rl", rb_)], writes=[("u", ub, f)])
                        drain(T_MD1)
                    for dc in range(NC8):
                        bank = psbank()

                        def mm(e, dc=dc, bank=bank, ub=ub, T=T, WD=WD):
                            for f in range(4):
                                i = e.matmul(PS[bank][:, 0:T], lhsT=WD[:, f, dc * 128:(dc + 1) * 128], rhs=U[ub][:, f, 0:T],
                                             start=(f == 0), stop=(f == 3))
                            return i
                        P.op("pe", mm, reads=[("wb", sd)] + [("u", ub, f) for f in range(4)], writes=[("ps", bank)])
                        if g == 0:
                            P.op("act", lambda e, dc=dc, bank=bank, mv=mv, T=T: e.activation(
                                out=mv[:, dc, :], in_=PS[bank][:, 0:T], func=AF.Copy),
                                reads=[("ps", bank)], writes=[("macc", sl, dc)])
                        else:
                            P.op("dve", lambda e, dc=dc, bank=bank, mv=mv, T=T: e.tensor_tensor(
                                out=mv[:, dc, :], in0=PS[bank][:, 0:T], in1=mv[:, dc, :], op=ALU.add),
                                reads=[("ps", bank), ("macc", sl, dc)], writes=[("macc", sl, dc)])
                        if g == 7:
                            xk_ = [("o", t) for t in range(5)] if sl != "C" else []
                            P.op("act", lambda e, dc=dc, mv=mv, T=T: e.activation(out=SQ[:, dc, 0:T], in_=mv[:, dc, :], func=AF.Square),
                                 reads=[("macc", sl, dc)] + xk_, writes=[("sq", dc)])
                        drain(T_MD2)
                    if g == 7:
                        flush()
                        post_head(T, SQ, RB)
                        if ti == 4:
                            P.barrier()
                        cnt_ = push(tail_steps(ti))
                        if ti == 1:
                            tailA[0] = cnt_

    finals = []
    P.barrier()
    for s in range(n_seq):
        xv = xT[s].rearrange("(c p) t -> p c t", p=128)
        for ti in (1, 2, 3, 4, 0):
            a, b = TT[ti]
            if ti == 0:
                P.op("sp", lambda e: e.dma_start(out=HRES[:, :, 0:NMETA], in_=metaT.rearrange("(c p) t -> p c t", p=128)),
                     writes=[("hres", 0)], dma=True, nobar=True, untracked=True)
            else:
                P.op("sp", lambda e, xv=xv, a=a, b=b: e.dma_start(out=HRES[:, :, a:b], in_=xv[:, :, a - NMETA:b - NMETA]),
                     writes=[("hres", ti)], dma=True, nobar=True, untracked=True)
        for n, li in enumerate(layers):
            nxt = layers[n + 1] if n + 1 < len(layers) else None
            if li % 2 == 0:
                phase_attention(li, n == 0)
            else:
                phase_conv(li, n == 0)
            phase_outproj(li)
            phase_mlp(li, nxt)
        flush()
        yv = yT[s].rearrange("(c p) t -> p c t", p=128)
        for ti in ((1, 2, 3, 4) if debug_out is None else (1, 2, 3, 4, 0)):
            a, b = TT[ti]
            if debug_out is None:
                finals.append(P.op("sp", lambda e, yv=yv, a=a, b=b: e.dma_start(out=yv[:, :, a - NMETA:b - NMETA], in_=HRES[:, :, a:b]),
                                   reads=[("hres", ti)], dma=True, nobar=True, untracked=True))
            else:
                finals.append(P.op("sp", lambda e, yv=yv, a=a, b=b: e.dma_start(out=yv[:, :, a:b], in_=HRES[:, :, a:b]),
                                   reads=[("hres", ti)], dma=True, nobar=True, untracked=True))
    P.emit(nc, es, finals)
    es.close()
    return nc


def _bf16_split(x):
    import ml_dtypes
    hi = x.astype(ml_dtypes.bfloat16).astype(np.float32)
    lo = (x - hi).astype(ml_dtypes.bfloat16).astype(np.float32)
    return hi, lo


def host_consts():
    pos = np.arange(L, dtype=np.float32)
    qaug = np.zeros((NH, 4, L), np.float32)
    kaug = np.zeros((NH, 4, L), np.float32)
    for h in range(NH):
        slope = 2.0 ** (-8.0 * (h + 1) / NH)
        x = 8.0 * slope * pos
        hi, lo = _bf16_split(x)
        qaug[h, 0], qaug[h, 1], qaug[h, 2], qaug[h, 3] = -hi, -lo, 1.0, 1.0
        kaug[h, 0], kaug[h, 1], kaug[h, 2], kaug[h, 3] = 1.0, 1.0, hi, lo
    tri = (np.arange(128)[:, None] <= np.arange(128)[None, :]).astype(np.float32)
    return qaug, kaug, tri


def prep_shared(meta_tokens, norm_g, w_in, w_out, lambda_params, subln_g, conv_w, w_up, w_down):
    qaug, kaug, tri = host_consts()
    cst = np.zeros((128, 192), np.float32)
    cst[:, 0:128] = norm_g.reshape(16, NC8, 128).transpose(2, 0, 1).reshape(128, 128)
    cst[:, 128:176] = conv_w.reshape(2, 3, NC8, 128).transpose(3, 0, 1, 2).reshape(128, 48)
    cst[:, 176:178] = subln_g.T
    lamp = np.ascontiguousarray(np.broadcast_to(lambda_params.reshape(1, 512), (128, 512)))
    w_in_g = np.ascontiguousarray(
        w_in.reshape(DEPTH, D, 3, NC8, 128).transpose(0, 3, 1, 2, 4).reshape(DEPTH, NC8, D, 384))
    return {
        "metaT": np.ascontiguousarray(meta_tokens.T), "cst": cst, "lamp": lamp, "tri": tri, "qaug": qaug, "kaug": kaug,
        "w_in_g": w_in_g, "w_out": np.ascontiguousarray(w_out), "w_up": np.ascontiguousarray(w_up),
        "w_down": np.ascontiguousarray(w_down),
    }


_NC_CACHE = {}


def kernel(x, meta_tokens, norm_g, w_in, w_out, lambda_params, subln_g, conv_w, w_up, w_down):
    x = np.asarray(x, np.float32)
    args = [np.asarray(a, np.float32) for a in (meta_tokens, norm_g, w_in, w_out, lambda_params, subln_g, conv_w, w_up, w_down)]
    shared = prep_shared(*args)
    n_cores = 8
    per = x.shape[0] // n_cores
    if "nc" not in _NC_CACHE:
        _NC_CACHE["nc"] = build(n_seq=per)
    nc = _NC_CACHE["nc"]
    in_maps = []
    for c in range(n_cores):
        m = dict(shared)
        m["xT"] = np.ascontiguousarray(x[c * per:(c + 1) * per].transpose(0, 2, 1))
        in_maps.append(m)
    res = run_bass_kernel_spmd(nc, in_maps, core_ids=list(range(n_cores)))
    out = np.empty_like(x)
    for c in range(n_cores):
        out[c * per:(c + 1) * per] = res.results[c]["yT"].transpose(0, 2, 1)
    return out
```

```python
import math
import os
import numpy as np
from contextlib import ExitStack
import concourse.bass as bass
import concourse.mybir as mybir
from concourse.bass_utils import run_bass_kernel_spmd

F32 = mybir.dt.float32
BF16 = mybir.dt.bfloat16
AF = mybir.ActivationFunctionType
ALU = mybir.AluOpType

D = 1024
NC8 = 8
SEQ = 2048
NMETA = 16
L = SEQ + NMETA
DEPTH = 4
NH = 8
DFF = 4096
EPS = 1e-6
TT = [(0, 16)] + [(16 + 512 * i, 16 + 512 * (i + 1)) for i in range(4)]
KT = [(0, 16)] + [(16 + 128 * j, 16 + 128 * (j + 1)) for j in range(16)]
ENGS = ("pe", "act", "dve", "pool", "sp")
NDMASEM = 24


def lam_init(i):
    return 0.8 - 0.6 * math.exp(-0.3 * i)


class Op:
    __slots__ = ("id", "eng", "fn", "deps", "dma", "marked", "count", "sem", "prev_dma")

    def __init__(self, id, eng, fn, deps, dma):
        self.id, self.eng, self.fn, self.deps, self.dma = id, eng, fn, deps, dma
        self.marked = False
        self.count = 0
        self.sem = None
        self.prev_dma = None


class Plan:
    def __init__(self):
        self.ops = []
        self.q = {e: [] for e in ENGS}
        self.lastw = {}
        self.readers = {}
        self.bar = set()
        self.since_bar = []
        self.dma_rr = 0
        self.dma_last = [None] * NDMASEM

    def op(self, eng, fn, reads=(), writes=(), dma=False, nobar=False, extra=(), untracked=False):
        deps = set(extra)
        for k in reads:
            w = self.lastw.get(k)
            if w is not None:
                deps.add(w)
        for k in writes:
            w = self.lastw.get(k)
            if w is not None:
                deps.add(w)
            for r in self.readers.get(k, ()):
                deps.add(r)
        if not nobar:
            deps |= self.bar
        o = Op(len(self.ops), eng, fn, deps, dma)
        if dma:
            o.sem = self.dma_rr
            o.prev_dma = self.dma_last[o.sem]
            self.dma_last[o.sem] = o.id
            self.dma_rr = (self.dma_rr + 1) % NDMASEM
        self.ops.append(o)
        self.q[eng].append(o)
        for k in reads:
            self.readers.setdefault(k, []).append(o.id)
        for k in writes:
            self.lastw[k] = o.id
            self.readers[k] = []
        if not untracked:
            self.since_bar.append(o.id)
        return o.id

    def barrier(self):
        last = {}
        for oid in self.since_bar:
            o = self.ops[oid]
            if o.dma:
                last[("dma", oid)] = oid
            else:
                last[o.eng] = oid
        self.bar = set(last.values())
        self.since_bar = list(self.bar)

    def emit(self, nc, es, final_ops):
        ops = self.ops
        for o in ops:
            for d in o.deps:
                ops[d].marked = True
            if o.prev_dma is not None:
                ops[o.prev_dma].marked = True
        for d in final_ops:
            ops[d].marked = True
        esem = {e: es.enter_context(nc.semaphore("sem_" + e)) for e in ("pe", "act", "dve", "pool")}
        dsem = [es.enter_context(nc.semaphore("sem_dma%d" % i)) for i in range(NDMASEM)]
        cnt = {e: 0 for e in ENGS}
        dcnt = [0] * NDMASEM
        for o in ops:
            if o.dma:
                dcnt[o.sem] += 16
                o.count = dcnt[o.sem]
            elif o.marked:
                cnt[o.eng] += 1
                o.count = cnt[o.eng]
        block = es.enter_context(nc.Block())

        def run(engname, e):
            waited = {}

            def wait_for(d):
                od = ops[d]
                if od.dma:
                    key, sem, val = ("d", od.sem), dsem[od.sem], od.count
                else:
                    if od.eng == "pe" and engname == "pe":
                        return
                    key, sem, val = ("e", od.eng), esem[od.eng], od.count
                if waited.get(key, 0) >= val:
                    return
                waited[key] = val
                e.wait_ge(sem, val)

            for o in self.q[engname]:
                for d in sorted(o.deps):
                    wait_for(d)
                if o.prev_dma is not None:
                    wait_for(o.prev_dma)
                ins = o.fn(e)
                if o.dma:
                    ins.then_inc(dsem[o.sem], 16)
                elif o.marked:
                    ins.then_inc(esem[o.eng], 1)
            if engname == "sp":
                for d in final_ops:
                    wait_for(d)

        @block.tensor
        def _(e):
            run("pe", e)

        @block.scalar
        def _(e):
            run("act", e)

        @block.vector
        def _(e):
            run("dve", e)

        @block.gpsimd
        def _(e):
            run("pool", e)

        @block.sync
        def _(e):
            run("sp", e)


ARENA_WORDS = 53100
_T = lambda k, d: int(os.environ.get("KT_" + k, d))
T_PARTB = _T("PARTB", 9)
T_OPD = _T("OPD", 2)
T_OPN = _T("OPN", 28)
T_MD1 = _T("MD1", 2)
T_MD2 = _T("MD2", 1)
T_MN = _T("MN", 24)
T_USQ = _T("USQ", 1)
T_PEND = _T("PEND", 3)
T_S3 = _T("S3", 1)
T_QKDVE = _T("QKDVE", 0)
T_RECIP = _T("RECIP", 0)
T_POOLMASK = _T("POOLMASK", 0)


def build(n_seq=4, layers=(0, 1, 2, 3), debug_out=None):
    nc = bass.Bass("TRN2", target_bir_lowering=False)
    xT = nc.dram_tensor("xT", [n_seq, D, SEQ], F32, kind="ExternalInput").ap()
    metaT = nc.dram_tensor("metaT", [D, NMETA], F32, kind="ExternalInput").ap()
    cst_d = nc.dram_tensor("cst", [128, 192], F32, kind="ExternalInput").ap()
    lam_d = nc.dram_tensor("lamp", [128, 512], F32, kind="ExternalInput").ap()
    tri_d = nc.dram_tensor("tri", [128, 128], F32, kind="ExternalInput").ap()
    qaug_d = nc.dram_tensor("qaug", [NH, 4, L], F32, kind="ExternalInput").ap()
    kaug_d = nc.dram_tensor("kaug", [NH, 4, L], F32, kind="ExternalInput").ap()
    w_in_d = nc.dram_tensor("w_in_g", [DEPTH, NC8, D, 384], F32, kind="ExternalInput").ap()
    w_out_d = nc.dram_tensor("w_out", [DEPTH, D, D], F32, kind="ExternalInput").ap()
    w_up_d = nc.dram_tensor("w_up", [DEPTH, D, DFF], F32, kind="ExternalInput").ap()
    w_dn_d = nc.dram_tensor("w_down", [DEPTH, DFF, D], F32, kind="ExternalInput").ap()
    yT = nc.dram_tensor("yT", [n_seq, D, SEQ if debug_out is None else L], F32, kind="ExternalOutput").ap()

    es = ExitStack()
    arena = es.enter_context(nc.sbuf_tensor("arena", [128, ARENA_WORDS], F32))
    PS = [es.enter_context(nc.psum_tensor("ps%d" % i, [128, 512], F32)) for i in range(8)]
    off = [0]

    def carve(words):
        a = off[0]
        off[0] += words
        assert off[0] <= ARENA_WORDS, off[0]
        return arena[:, a:a + words]

    def bf(ap):
        return ap.bitcast(BF16)

    HRES = carve(NC8 * L).rearrange("p (c t) -> p c t", c=NC8)
    H = bf(carve(NC8 * L // 2)).rearrange("p (c t) -> p c t", c=NC8)
    OREG = carve(8320)
    O = bf(OREG[:, 0:NC8 * L // 2]).rearrange("p (c t) -> p c t", c=NC8)
    MACC = OREG.rearrange("p (c t) -> p c t", c=NC8)
    WB = [bf(carve(2048)) for _ in range(4)]
    CST = carve(192)
    ONES = bf(carve(64))
    TRI = bf(carve(64))
    LAMS = carve(16)
    SGS = carve(2)
    EPSB = carve(2)
    SQP = bf(carve(NC8 * 256)).rearrange("p (c t) -> p c t", c=NC8)
    RBP = carve(512)
    RB2 = carve(512)
    scratch0 = off[0]

    G = CST[:, 0:128].rearrange("p (a c) -> p a c", c=NC8)
    CW = CST[:, 128:176].rearrange("p (a k c) -> p a k c", a=2, k=3)
    SG = CST[:, 176:178]

    P = Plan()
    wb_rr = [0]

    def wslot():
        s = wb_rr[0]
        wb_rr[0] = (s + 1) % 4
        return s

    ps_rr = [0]

    def psbank(lo=0, hi=8):
        b = lo + ps_rr[0] % (hi - lo)
        ps_rr[0] += 1
        return b

    P.op("sp", lambda e: e.dma_start(out=CST, in_=cst_d), writes=["cst"], dma=True)
    LAMP_raw = arena[:, ARENA_WORDS - 512:ARENA_WORDS]
    LAMP = LAMP_raw.rearrange("p (a k d) -> p a k d", a=2, k=4)
    P.op("sp", lambda e: e.dma_start(out=LAMP_raw, in_=lam_d), writes=["lamp"], dma=True)
    P.op("pool", lambda e: e.dma_start(out=TRI, in_=tri_d), writes=["tri"], dma=True)
    P.op("dve", lambda e: e.memset(ONES, 1.0), writes=["ones"])
    P.op("dve", lambda e: e.memset(EPSB, EPS), writes=["epsb"])

    def lam_setup(a, li):
        t = scratch_view(0, 256)
        P.op("dve", lambda e: e.tensor_tensor(out=t[:, 0:64], in0=LAMP[:, a, 0, :], in1=LAMP[:, a, 1, :], op=ALU.mult),
             reads=["lamp"], writes=["lamt0"])
        P.op("dve", lambda e: e.tensor_tensor(out=t[:, 64:128], in0=LAMP[:, a, 2, :], in1=LAMP[:, a, 3, :], op=ALU.mult),
             reads=["lamp"], writes=["lamt1"])
        P.op("dve", lambda e: e.reduce_sum(out=t[:, 128:129], in_=t[:, 0:64], axis=mybir.AxisListType.X),
             reads=["lamt0"], writes=["lamt2"])
        P.op("dve", lambda e: e.reduce_sum(out=t[:, 129:130], in_=t[:, 64:128], axis=mybir.AxisListType.X),
             reads=["lamt1"], writes=["lamt3"])
        P.op("act", lambda e: e.activation(out=t[:, 130:132], in_=t[:, 128:130], func=AF.Exp),
             reads=["lamt2", "lamt3"], writes=["lamt4"])
        P.op("dve", lambda e: e.scalar_tensor_tensor(out=LAMS[:, 4 * a:4 * a + 1], in0=t[:, 131:132],
                                                     scalar=-lam_init(li), in1=t[:, 130:131],
                                                     op0=ALU.add, op1=ALU.subtract),
             reads=["lamt4"], writes=["lams%d" % a])
        P.op("dve", lambda e: e.tensor_scalar(out=SGS[:, a:a + 1], in0=SG[:, a:a + 1], scalar1=1.0 - lam_init(li),
                                              scalar2=None, op0=ALU.mult),
             reads=["cst"], writes=["sgs%d" % a])

    def scratch_view(o, words):
        assert scratch0 + o + words <= ARENA_WORDS, (o, words)
        return arena[:, scratch0 + o:scratch0 + o + words]

    for a, li in ((0, 0), (1, 2)):
        lam_setup(a, li)
    P.barrier()

    def load_w(slot, dram_ap, shape3):
        kc, cols = shape3
        dst = WB[slot][:, 0:kc * cols].rearrange("p (k c) -> p k c", k=kc)
        P.op("pool", lambda e: e.dma_start(out=dst, in_=dram_ap.rearrange("(k p) c -> p k c", p=128)),
             writes=[("wb", slot)], dma=True, nobar=True)
        return dst

    bg = []
    bgc = [0, 0]

    def push(steps):
        bg.extend(steps)
        bgc[0] += len(steps)
        return bgc[0]

    def drain(n=1):
        for _ in range(n):
            if bg:
                bg.pop(0)()
                bgc[1] += 1

    def flush_to(target):
        while bg and bgc[1] < target:
            bg.pop(0)()
            bgc[1] += 1

    def flush():
        flush_to(bgc[0])

    def run_steps(steps):
        for st in steps:
            st()

    def norm_stats_steps(src3, a, b, SQ, RB, inv_n, keys_r, banks=(0, 8)):
        T = b - a
        cell = {}

        def s1():
            P.op("act", lambda e: e.activation(out=SQ[:, :, 0:T], in_=src3, func=AF.Square),
                 reads=keys_r, writes=["sq"] + sqk)

        def s2():
            bank = psbank(*banks)

            def mm(e):
                for c in range(NC8):
                    i = e.matmul(PS[bank][:, 0:T], lhsT=ONES, rhs=SQ[:, c, 0:T], start=(c == 0), stop=(c == NC8 - 1))
                return i
            P.op("pe", mm, reads=["sq", "ones"], writes=[("ps", bank)])
            P.op("act", lambda e: e.activation(out=RB[:, 0:T], in_=PS[bank][:, 0:T], func=AF.Ln, scale=inv_n, bias=EPSB[:, 0:1]),
                 reads=[("ps", bank), "epsb"], writes=["rb"])
            P.op("act", lambda e: e.activation(out=RB[:, 0:T], in_=RB[:, 0:T], func=AF.Exp, scale=-0.5),
                 reads=["rb"], writes=["rb"])
        nop = lambda: None
        return [s1, nop, nop, nop, nop, nop, nop, s2, nop, nop, nop]

    def pre_norm_steps(li, gi, ti, SQ, RB, banks=(0, 8)):
        a, b = TT[ti]
        T = b - a
        steps = norm_stats_steps(HRES[:, :, a:b], a, b, SQ, RB, 1.0 / D, [("hres", ti)], banks)

        def mk(c):
            def st():
                P.op("dve", lambda e: e.scalar_tensor_tensor(
                    out=H[:, c, a:b], in0=HRES[:, c, a:b], scalar=G[:, 4 * li + gi, c:c + 1], in1=RB[:, 0:T],
                    op0=ALU.mult, op1=ALU.mult),
                    reads=[("hres", ti), "rb", "cst"], writes=[("h", ti)])
            return st
        return steps + [mk(c) for c in range(NC8)]

    def pre_norm(li, gi, ti, SQ, RB, banks=(0, 8)):
        run_steps(pre_norm_steps(li, gi, ti, SQ, RB, banks))

    def post_norm_steps(li, gi, ti, M3, mkeys, SQ, RB, banks=(0, 8), xkeys=()):
        a, b = TT[ti]
        T = b - a
        steps = norm_stats_steps(M3, a, b, SQ, RB, 1.0 / D, list(mkeys) + list(xkeys), banks)

        def mk(c):
            def st():
                P.op("pool", lambda e: e.tensor_tensor(out=M3[:, c, :], in0=M3[:, c, :], in1=RB[:, 0:T], op=ALU.mult),
                     reads=[mkeys[c], "rb"] + list(xkeys), writes=[mkeys[c]])
                P.op("dve", lambda e: e.scalar_tensor_tensor(
                    out=HRES[:, c, a:b], in0=M3[:, c, :], scalar=G[:, 4 * li + gi, c:c + 1], in1=HRES[:, c, a:b],
                    op0=ALU.mult, op1=ALU.add),
                    reads=[mkeys[c], "cst", ("hres", ti)] + list(xkeys), writes=[("hres", ti)])
            return st
        return steps + [mk(c) for c in range(NC8)]

    sqk = [("sq", c) for c in range(NC8)]

    def post_head(T, SQ, RB, banks=(0, 8)):
        bank = psbank(*banks)

        def mm(e):
            for c in range(NC8):
                i = e.matmul(PS[bank][:, 0:T], lhsT=ONES, rhs=SQ[:, c, 0:T], start=(c == 0), stop=(c == NC8 - 1))
            return i
        P.op("pe", mm, reads=sqk + ["sq", "ones"], writes=[("ps", bank)])
        P.op("act", lambda e: e.activation(out=RB[:, 0:T], in_=PS[bank][:, 0:T], func=AF.Ln, scale=1.0 / D, bias=EPSB[:, 0:1]),
             reads=[("ps", bank), "epsb"], writes=["rb"])
        P.op("act", lambda e: e.activation(out=RB[:, 0:T], in_=RB[:, 0:T], func=AF.Exp, scale=-0.5),
             reads=["rb"], writes=["rb"])

    def post_tail_steps(li, gi, ti, M3, mkeys, RB, nxt, banks=(0, 8), xkeys=(), nops=T_MN):
        a, b = TT[ti]
        T = b - a
        xk = list(xkeys)

        def s0():
            for c in [c for c in range(NC8) if (T_POOLMASK >> c) & 1]:
                P.op("pool", lambda e, c=c: e.tensor_tensor(out=M3[:, c, :], in0=M3[:, c, :], in1=RB[:, 0:T], op=ALU.mult),
                     reads=[mkeys[c], "rb"] + xk, writes=[mkeys[c]])

        def mk(c):
            def st():
                if not (T_POOLMASK >> c) & 1:
                    P.op("dve", lambda e: e.tensor_tensor(out=M3[:, c, :], in0=M3[:, c, :], in1=RB[:, 0:T], op=ALU.mult),
                         reads=[mkeys[c], "rb"] + xk, writes=[mkeys[c]])
                P.op("dve", lambda e: e.scalar_tensor_tensor(
                    out=HRES[:, c, a:b], in0=M3[:, c, :], scalar=G[:, 4 * li + gi, c:c + 1], in1=HRES[:, c, a:b],
                    op0=ALU.mult, op1=ALU.add),
                    reads=[mkeys[c], "cst", ("hres", ti)] + xk, writes=[("hres", ti)])
                if nxt is not None:
                    P.op("act", lambda e: e.activation(out=H[:, c, a:b], in_=HRES[:, c, a:b], func=AF.Square),
                         reads=[("hres", ti)], writes=[("h", ti)])
            return st
        steps = [s0] + [mk(c) for c in (0, 2, 4, 6, 1, 3, 5, 7)]
        if nxt is None:
            return steps
        li2, gi2 = nxt

        def s2():
            bank = psbank(*banks)

            def mm(e):
                for c in range(NC8):
                    i = e.matmul(PS[bank][:, 0:T], lhsT=ONES, rhs=H[:, c, a:b], start=(c == 0), stop=(c == NC8 - 1))
                return i
            P.op("pe", mm, reads=[("h", ti), "ones"], writes=[("ps", bank)])
            P.op("act", lambda e: e.activation(out=RB2[:, 0:T], in_=PS[bank][:, 0:T], func=AF.Ln, scale=1.0 / D, bias=EPSB[:, 0:1]),
                 reads=[("ps", bank), "epsb"], writes=["rb2"])
            P.op("act", lambda e: e.activation(out=RB2[:, 0:T], in_=RB2[:, 0:T], func=AF.Exp, scale=-0.5),
                 reads=["rb2"], writes=["rb2"])

        def mk2(c):
            def st():
                P.op("dve", lambda e: e.scalar_tensor_tensor(
                    out=H[:, c, a:b], in0=HRES[:, c, a:b], scalar=G[:, 4 * li2 + gi2, c:c + 1], in1=RB2[:, 0:T],
                    op0=ALU.mult, op1=ALU.mult),
                    reads=[("hres", ti), "rb2", "cst"], writes=[("h", ti)])
            return st
        nop = lambda: None
        return steps + [nop] * nops + [s2, nop, nop] + [mk2(c) for c in range(NC8)]

    def phase_attention(li, do_pre):
        a_idx = li // 2
        so = [0]

        def sv(words):
            v = scratch_view(so[0], words)
            so[0] += words
            return v
        QA, QB, KA, KB = [bf(sv(L // 2)) for _ in range(4)]
        VH = bf(sv(17 * 64)).rearrange("p (j e) -> p j e", j=17)
        PT = [bf(sv(256)) for _ in range(3)] + [bf(RB2[:, 0:256])]
        npt = 4 if T_S3 else 3
        R0s, R1s = [sv(512), sv(512)], [sv(512), sv(512)]
        SQ1s = [bf(sv(256))] * 2
        fin_rr = [0]
        pendB = []
        SQ, RB = SQP, RBP
        if do_pre:
            for ti in range(5):
                pre_norm(li, 0, ti, SQ, RB)
        pt_rr = [0]
        for h in range(NH):
            slot = wslot()
            W = load_w(slot, w_in_d[li, h], (NC8, 384))
            for dst, src, key in ((QA, qaug_d, "qaug"), (QB, qaug_d, "qaug2"), (KA, kaug_d, "kaug"), (KB, kaug_d, "kaug2")):
                P.op("pool", lambda e, dst=dst, src=src, h=h: e.dma_start(out=dst[64:68, :], in_=src[h]),
                     writes=[key], dma=True)
            for ti, (a, b) in enumerate(TT):
                T = b - a
                if ti == 4:
                    flush()
                for which, (XA, XB) in enumerate(((QA, QB), (KA, KB))):
                    drain(5)
                    bank = 6 + (ps_rr[0] % 2)
                    ps_rr[0] += 1

                    def mm(e, which=which, bank=bank, a=a, b=b, T=T, W=W):
                        for kc in range(NC8):
                            i = e.matmul(PS[bank][:, 0:T], lhsT=W[:, kc, which * 128:(which + 1) * 128],
                                         rhs=H[:, kc, a:b], start=(kc == 0), stop=(kc == NC8 - 1))
                        return i
                    P.op("pe", mm, reads=[("wb", slot), ("h", ti)], writes=[("ps", bank)])
                    nm = "qk"[which]
                    if T_QKDVE:
                        P.op("dve", lambda e, XA=XA, bank=bank, a=a, b=b, T=T: e.tensor_copy(
                            out=XA[0:64, a:b], in_=PS[bank][0:64, 0:T]),
                            reads=[("ps", bank)], writes=[(nm + "a", ti)])
                    else:
                        P.op("act", lambda e, XA=XA, bank=bank, a=a, b=b, T=T: e.activation(
                            out=XA[0:64, a:b], in_=PS[bank][0:64, 0:T], func=AF.Copy),
                            reads=[("ps", bank)], writes=[(nm + "a", ti)])
                    P.op("act", lambda e, XB=XB, bank=bank, a=a, b=b, T=T: e.activation(
                        out=XB[0:64, a:b], in_=PS[bank][64:128, 0:T], func=AF.Copy),
                        reads=[("ps", bank)], writes=[(nm + "b", ti)])
            groups = [[0]] + [[1 + 4 * g + m for m in range(4)] for g in range(4)]
            for gi, grp in enumerate(groups):
                bank = 6 + (ps_rr[0] % 2)
                ps_rr[0] += 1

                def mm(e, grp=grp, bank=bank, W=W):
                    for m, j in enumerate(grp):
                        ka, kb = KT[j]
                        for kc in range(NC8):
                            i = e.matmul(PS[bank][0:kb - ka, 128 * m:128 * (m + 1)], lhsT=H[:, kc, ka:kb],
                                         rhs=W[:, kc, 256:384], start=(kc == 0), stop=(kc == NC8 - 1))
                    return i
                P.op("pe", mm, reads=[("wb", slot), ("h", gi)], writes=[("ps", bank)])
                if gi == 0:
                    P.op("dve", lambda e, bank=bank: e.tensor_copy(out=VH[0:16, 0, :], in_=PS[bank][0:16, 0:128]),
                         reads=[("ps", bank)], writes=[("v", gi)])
                else:
                    j0 = grp[0]
                    P.op("dve", lambda e, bank=bank, j0=j0: e.tensor_copy(
                        out=VH[:, j0:j0 + 4, :], in_=PS[bank][:, :].rearrange("p (j e) -> p j e", j=4)),
                        reads=[("ps", bank)], writes=[("v", gi)])
            def attend_tile(h, qi):
                qa, qb = TT[qi]
                Tq = qb - qa
                jmax = 0 if qi == 0 else 4 * qi
                units = [(j, c) for j in range(jmax + 1) for c in range(2)]
                pend = []

                def issue_score(j, c):
                    ka, kb = KT[j]
                    kk = kb - ka
                    diag = (qi == 0) or (j > 4 * (qi - 1))
                    c0 = 0 if (qi == 0 or not diag) else 128 * (j - 4 * (qi - 1) - 1)
                    sbank = (0, 1, 7)[ps_rr[0] % 3] if T_S3 else ps_rr[0] % 2
                    ps_rr[0] += 1
                    pslot = pt_rr[0] % npt
                    xk = ["rb2"] if pslot == 3 else []
                    pt_rr[0] += 1
                    Kt, Qt = (KA, QA) if c == 0 else (KB, QB)
                    kti = 0 if j == 0 else 1 + (j - 1) // 4
                    sfx = "a" if c == 0 else "b"
                    P.op("pe", lambda e: e.matmul(PS[sbank][0:kk, c0:Tq], lhsT=Kt[0:68, ka:kb], rhs=Qt[0:68, qa + c0:qb],
                                                  start=True, stop=True),
                         reads=[("k" + sfx, kti), ("q" + sfx, qi), "kaug" + ("" if c == 0 else "2"),
                                "qaug" + ("" if c == 0 else "2")],
                         writes=[("ps", sbank)])
                    P.op("act", lambda e: e.activation(out=PT[pslot][0:kk, c0:Tq], in_=PS[sbank][0:kk, c0:Tq],
                                                       func=AF.Exp, scale=0.125),
                         reads=[("ps", sbank)], writes=[("pt", pslot)] + xk)
                    if diag:
                        w = min(128, Tq - c0)
                        P.op("dve", lambda e: e.tensor_tensor(out=PT[pslot][0:kk, c0:c0 + w], in0=PT[pslot][0:kk, c0:c0 + w],
                                                              in1=TRI[0:kk, 0:w], op=ALU.mult),
                             reads=[("pt", pslot), "tri"], writes=[("pt", pslot)] + xk)
                    return (j, c, kk, c0, pslot)

                def issue_pv(u):
                    j, c, kk, c0, pslot = u
                    first, last = (j == 0), (j == jmax)
                    kti = 0 if j == 0 else 1 + (j - 1) // 4

                    def mm(e):
                        e.matmul(PS[2 + c][:, c0:Tq], lhsT=VH[0:kk, j, :], rhs=PT[pslot][0:kk, c0:Tq], start=first, stop=last)
                        return e.matmul(PS[4 + c][:, c0:Tq], lhsT=ONES[0:kk, :], rhs=PT[pslot][0:kk, c0:Tq], start=first, stop=last)
                    P.op("pe", mm, reads=[("pt", pslot), ("v", kti), "ones"] + (["rb2"] if pslot == 3 else []),
                         writes=[("ps", 2 + c), ("ps", 4 + c)])

                for n_u, (j, c) in enumerate(units):
                    pend.append(issue_score(j, c))
                    if len(pend) > T_PEND:
                        issue_pv(pend.pop(0))
                    if n_u == T_PARTB and pendB:
                        pendB.pop(0)()
                while pend:
                    issue_pv(pend.pop(0))
                fb = fin_rr[0] % 2
                fin_rr[0] += 1
                R0, R1, SQ1 = R0s[fb], R1s[fb], SQ1s[fb]
                kr0, kr1, ksq = ("r0", fb), ("r1", fb), ("sq1", 0)
                for Rx, kx, sb in ((R0, kr0, 4), (R1, kr1, 5)):
                    if T_RECIP:
                        P.op("dve", lambda e, Rx=Rx, sb=sb: e.reciprocal(out=Rx[:, 0:Tq], in_=PS[sb][:, 0:Tq]),
                             reads=[("ps", sb)], writes=[kx])
                        continue
                    P.op("act", lambda e, Rx=Rx, sb=sb: e.activation(out=Rx[:, 0:Tq], in_=PS[sb][:, 0:Tq], func=AF.Ln),
                         reads=[("ps", sb)], writes=[kx])
                    P.op("act", lambda e, Rx=Rx: e.activation(out=Rx[:, 0:Tq], in_=Rx[:, 0:Tq], func=AF.Exp, scale=-1.0),
                         reads=[kx], writes=[kx])
                P.op("dve", lambda e: e.tensor_tensor(out=R0[:, 0:Tq], in0=PS[2][:, 0:Tq], in1=R0[:, 0:Tq], op=ALU.mult),
                     reads=[("ps", 2), kr0], writes=[kr0])
                P.op("dve", lambda e: e.tensor_tensor(out=R1[:, 0:Tq], in0=PS[3][:, 0:Tq], in1=R1[:, 0:Tq], op=ALU.mult),
                     reads=[("ps", 3), kr1], writes=[kr1])

                def partB():
                    P.op("dve", lambda e: e.scalar_tensor_tensor(out=R0[:, 0:Tq], in0=R1[:, 0:Tq], scalar=LAMS[:, 4 * a_idx:4 * a_idx + 1],
                                                                 in1=R0[:, 0:Tq], op0=ALU.mult, op1=ALU.add),
                         reads=[kr0, kr1, "lams%d" % a_idx], writes=[kr0])
                    P.op("dve", lambda e: e.tensor_tensor(out=SQ1[:, 0:Tq], in0=R0[:, 0:Tq], in1=R0[:, 0:Tq], op=ALU.mult), reads=[kr0], writes=[ksq])
                    sbank = 6 if T_S3 else 6 + ps_rr[0] % 2
                    ps_rr[0] += 1
                    P.op("pe", lambda e: e.matmul(PS[sbank][:, 0:Tq], lhsT=ONES, rhs=SQ1[:, 0:Tq], start=True, stop=True),
                         reads=[ksq, "ones"], writes=[("ps", sbank)])
                    P.op("act", lambda e: e.activation(out=R1[:, 0:Tq], in_=PS[sbank][:, 0:Tq], func=AF.Ln, scale=1.0 / 128, bias=EPSB[:, 0:1]),
                         reads=[("ps", sbank), "epsb"], writes=[kr1])
                    P.op("act", lambda e: e.activation(out=R1[:, 0:Tq], in_=R1[:, 0:Tq], func=AF.Exp, scale=-0.5),
                         reads=[kr1], writes=[kr1])
                    P.op("dve", lambda e: e.scalar_tensor_tensor(out=O[:, h, qa:qb], in0=R0[:, 0:Tq], scalar=SGS[:, a_idx:a_idx + 1],
                                                                 in1=R1[:, 0:Tq], op0=ALU.mult, op1=ALU.mult),
                         reads=[kr0, kr1, "sgs%d" % a_idx], writes=[("o", qi)])
                pendB.append(partB)

            for qi in (1, 2, 3, 4, 0):
                attend_tile(h, qi)
                if qi == 0:
                    while pendB:
                        pendB.pop(0)()
        while pendB:
            pendB.pop(0)()
        P.barrier()

    def phase_conv(li, do_pre):
        ci = li // 2
        so = [0]

        def sv(words):
            v = scratch_view(so[0], words)
            so[0] += words
            return v
        BG = sv(L)
        UP = sv(L + 2)
        Y = sv(L)
        CT = sv(512)
        SQ, RB = SQP, RBP
        P.op("dve", lambda e: e.memset(UP[:, 0:2], 0.0), writes=["up0"])
        if do_pre:
            for ti in range(5):
                pre_norm(li, 0, ti, SQ, RB)
        for j in range(NC8):
            slot = wslot()
            W = load_w(slot, w_in_d[li, j], (NC8, 384))
            for ti, (a, b) in enumerate(TT):
                T = b - a
                banks = []
                if ti == 4:
                    flush()
                for which in range(3):
                    drain(4)
                    bank = psbank()
                    banks.append(bank)

                    def mm(e, which=which, bank=bank, a=a, b=b, T=T, W=W):
                        for kc in range(NC8):
                            i = e.matmul(PS[bank][:, 0:T], lhsT=W[:, kc, which * 128:(which + 1) * 128],
                                         rhs=H[:, kc, a:b], start=(kc == 0), stop=(kc == NC8 - 1))
                        return i
                    P.op("pe", mm, reads=[("wb", slot), ("h", ti)], writes=[("ps", bank)])
                P.op("act", lambda e, bank=banks[0], a=a, b=b, T=T: e.activation(out=BG[:, a:b], in_=PS[bank][:, 0:T], func=AF.Copy),
                     reads=[("ps", banks[0])], writes=[("bg", ti)])
                P.op("act", lambda e, bank=banks[1], T=T: e.activation(out=CT[:, 0:T], in_=PS[bank][:, 0:T], func=AF.Copy),
                     reads=[("ps", banks[1])], writes=["ct"])
                P.op("dve", lambda e, bank=banks[2], a=a, b=b, T=T: e.tensor_tensor(
                    out=UP[:, 2 + a:2 + b], in0=PS[bank][:, 0:T], in1=CT[:, 0:T], op=ALU.mult),
                    reads=[("ps", banks[2]), "ct"], writes=[("up", ti)])
            upk = [("up", t) for t in range(5)] + ["up0"]
            P.op("dve", lambda e, j=j: e.tensor_scalar(out=Y, in0=UP[:, 2:L + 2], scalar1=CW[:, ci, 2, j:j + 1], scalar2=None, op0=ALU.mult),
                 reads=upk + ["cst"], writes=["y"])
            P.op("dve", lambda e, j=j: e.scalar_tensor_tensor(out=Y, in0=UP[:, 1:L + 1], scalar=CW[:, ci, 1, j:j + 1], in1=Y,
                                                              op0=ALU.mult, op1=ALU.add),
                 reads=upk + ["cst", "y"], writes=["y"])
            P.op("dve", lambda e, j=j: e.scalar_tensor_tensor(out=Y, in0=UP[:, 0:L], scalar=CW[:, ci, 0, j:j + 1], in1=Y,
                                                              op0=ALU.mult, op1=ALU.add),
                 reads=upk + ["cst", "y"], writes=["y"])
            P.op("dve", lambda e, j=j: e.tensor_tensor(out=O[:, j, :], in0=BG, in1=Y, op=ALU.mult),
                 reads=["y"] + [("bg", t) for t in range(5)], writes=[("o", t) for t in range(5)])
        P.barrier()

    def phase_outproj(li):
        so = [0]

        def sv(words):
            v = scratch_view(so[0], words)
            so[0] += words
            return v
        Ms = [sv(NC8 * 512).rearrange("p (c t) -> p c t", c=NC8) for _ in range(2)]
        SQ, RB = SQP, RBP
        s0, s1 = wslot(), wslot()
        W0 = load_w(s0, w_out_d[li][:, 0:512], (NC8, 512))
        W1 = load_w(s1, w_out_d[li][:, 512:1024], (NC8, 512))

        order = [1, 2, 3, 4, 0]
        for n, ti in enumerate(order):
            a, b = TT[ti]
            T = b - a
            mb = n % 2
            for dc in range(NC8):
                bank = psbank(0, 6)
                Wx, sl, col = (W0, s0, dc * 128) if dc < 4 else (W1, s1, (dc - 4) * 128)

                def mm(e, Wx=Wx, col=col, bank=bank, a=a, b=b, T=T):
                    for kc in range(NC8):
                        i = e.matmul(PS[bank][:, 0:T], lhsT=Wx[:, kc, col:col + 128], rhs=O[:, kc, a:b],
                                     start=(kc == 0), stop=(kc == NC8 - 1))
                    return i
                P.op("pe", mm, reads=[("wb", sl), ("o", ti)], writes=[("ps", bank)])
                P.op("act", lambda e, dc=dc, bank=bank, T=T, mb=mb: e.activation(out=Ms[mb][:, dc, 0:T], in_=PS[bank][:, 0:T], func=AF.Copy),
                     reads=[("ps", bank)], writes=[("m", mb, dc)])
                P.op("act", lambda e, dc=dc, bank=bank, T=T: e.activation(out=SQ[:, dc, 0:T], in_=PS[bank][:, 0:T], func=AF.Square),
                     reads=[("ps", bank)], writes=[("sq", dc)])
                drain(T_OPD)
            flush()
            post_head(T, SQ, RB, banks=(6, 8))
            push(post_tail_steps(li, 1, ti, Ms[mb][:, :, 0:T], [("m", mb, dc) for dc in range(NC8)], RB, (li, 2), banks=(6, 8), nops=T_OPN))
        flush()
        P.barrier()

    def phase_mlp(li, next_li):
        so = [0]

        def sv(words):
            v = scratch_view(so[0], words)
            so[0] += words
            return v
        U = [bf(sv(1024)).rearrange("p (f t) -> p f t", f=4) for _ in range(2)]
        RL = [sv(512) for _ in range(2)]
        SQ, RB = SQP, RBP
        MC = sv(NC8 * 512).rearrange("p (c t) -> p c t", c=NC8)
        halves = [[0, 1, 2], [3, 4]]
        mview = {0: ("Z", MACC[:, :, 0:16]), 1: ("A", MACC[:, :, 16:528]), 2: ("B", MACC[:, :, 528:1040]),
                 3: ("C", MC[:, :, :]), 4: ("A", MACC[:, :, 16:528])}
        u_rr = [0]
        rl_rr = [0]
        tailA = [0]

        def tail_steps(ti):
            sl, mv = mview[ti]
            xk = [("o", t) for t in range(5)] if sl != "C" else []
            return post_tail_steps(li, 3, ti, mv, [("macc", sl, dc) for dc in range(NC8)], RB,
                                   (next_li, 0) if next_li is not None else None, xkeys=xk)

        for hi, tiles in enumerate(halves):
            for g in range(8):
                su, sd = wslot(), wslot()
                WU = load_w(su, w_up_d[li][:, g * 512:(g + 1) * 512], (NC8, 512))
                WD = load_w(sd, w_dn_d[li][g * 512:(g + 1) * 512, :], (4, 1024))
                for n, ti in enumerate(tiles):
                    a, b = TT[ti]
                    T = b - a
                    sl, mv = mview[ti]
                    ub = u_rr[0] % 2
                    u_rr[0] += 1
                    if hi == 1 and g == 0 and ti == 4:
                        flush_to(tailA[0])
                    for f in range(4):
                        bank = psbank()

                        def mm(e, f=f, bank=bank, a=a, b=b, T=T, WU=WU):
                            for kc in range(NC8):
                                i = e.matmul(PS[bank][:, 0:T], lhsT=WU[:, kc, f * 128:(f + 1) * 128], rhs=H[:, kc, a:b],
                                             start=(kc == 0), stop=(kc == NC8 - 1))
                            return i
                        P.op("pe", mm, reads=[("wb", su), ("h", ti)], writes=[("ps", bank)])
                        rb_ = rl_rr[0] % 2
                        rl_rr[0] += 1
                        P.op("act", lambda e, bank=bank, rb_=rb_, T=T: e.activation(out=RL[rb_][:, 0:T], in_=PS[bank][:, 0:T], func=AF.Relu),
                             reads=[("ps", bank)], writes=[("rl", rb_)])
                        if T_USQ:
                            P.op("act", lambda e, f=f, ub=ub, rb_=rb_, T=T: e.activation(
                                out=U[ub][:, f, 0:T], in_=RL[rb_][:, 0:T], func=AF.Square),
                                reads=[("rl", rb_)], writes=[("u", ub, f)])
                        else:
                            P.op("dve", lambda e, f=f, ub=ub, rb_=rb_, T=T: e.tensor_tensor(
                                out=U[ub][:, f, 0:T], in0=RL[rb_][:, 0:T], in1=RL[rb_][:, 0:T], op=ALU.mult),
                                reads=[("rl", rb_)], writes=[("u", ub, f)])
                        drain(T_MD1)
                    for dc in range(NC8):
                        bank = psbank()

                        def mm(e, dc=dc, bank=bank, ub=ub, T=T, WD=WD):
                            for f in range(4):
                                i = e.matmul(PS[bank][:, 0:T], lhsT=WD[:, f, dc * 128:(dc + 1) * 128], rhs=U[ub][:, f, 0:T],
                                             start=(f == 0), stop=(f == 3))
                            return i
                        P.op("pe", mm, reads=[("wb", sd)] + [("u", ub, f) for f in range(4)], writes=[("ps", bank)])
                        if g == 0:
                            P.op("act", lambda e, dc=dc, bank=bank, mv=mv, T=T: e.activation(
                                out=mv[:, dc, :], in_=PS[bank][:, 0:T], func=AF.Copy),
                                reads=[("ps", bank)], writes=[("macc", sl, dc)])
                        else:
                            P.op("dve", lambda e, dc=dc, bank=bank, mv=mv, T=T: e.tensor_tensor(
                                out=mv[:, dc, :], in0=PS[bank][:, 0:T], in1=mv[:, dc, :], op=ALU.add),
                                reads=[("ps", bank), ("macc", sl, dc)], writes=[("macc", sl, dc)])
                        if g == 7:
                            xk_ = [("o", t) for t in range(5)] if sl != "C" else []
                            P.op("act", lambda e, dc=dc, mv=mv, T=T: e.activation(out=SQ[:, dc, 0:T], in_=mv[:, dc, :], func=AF.Square),
                                 reads=[("macc", sl, dc)] + xk_, writes=[("sq", dc)])
                        drain(T_MD2)
                    if g == 7:
                        flush()
                        post_head(T, SQ, RB)
                        if ti == 4:
                            P.barrier()
                        cnt_ = push(tail_steps(ti))
                        if ti == 1:
                            tailA[0] = cnt_

    finals = []
    P.barrier()
    for s in range(n_seq):
        xv = xT[s].rearrange("(c p) t -> p c t", p=128)
        for ti in (1, 2, 3, 4, 0):
            a, b = TT[ti]
            if ti == 0:
                P.op("sp", lambda e: e.dma_start(out=HRES[:, :, 0:NMETA], in_=metaT.rearrange("(c p) t -> p c t", p=128)),
                     writes=[("hres", 0)], dma=True, nobar=True, untracked=True)
            else:
                P.op("sp", lambda e, xv=xv, a=a, b=b: e.dma_start(out=HRES[:, :, a:b], in_=xv[:, :, a - NMETA:b - NMETA]),
                     writes=[("hres", ti)], dma=True, nobar=True, untracked=True)
        for n, li in enumerate(layers):
            nxt = layers[n + 1] if n + 1 < len(layers) else None
            if li % 2 == 0:
                phase_attention(li, n == 0)
            else:
                phase_conv(li, n == 0)
            phase_outproj(li)
            phase_mlp(li, nxt)
        flush()
        yv = yT[s].rearrange("(c p) t -> p c t", p=128)
        for ti in ((1, 2, 3, 4) if debug_out is None else (1, 2, 3, 4, 0)):
            a, b = TT[ti]
            if debug_out is None:
                finals.append(P.op("sp", lambda e, yv=yv, a=a, b=b: e.dma_start(out=yv[:, :, a - NMETA:b - NMETA], in_=HRES[:, :, a:b]),
                                   reads=[("hres", ti)], dma=True, nobar=True, untracked=True))
            else:
                finals.append(P.op("sp", lambda e, yv=yv, a=a, b=b: e.dma_start(out=yv[:, :, a:b], in_=HRES[:, :, a:b]),
                                   reads=[("hres", ti)], dma=True, nobar=True, untracked=True))
    P.emit(nc, es, finals)
    es.close()
    return nc


def _bf16_split(x):
    import ml_dtypes
    hi = x.astype(ml_dtypes.bfloat16).astype(np.float32)
    lo = (x - hi).astype(ml_dtypes.bfloat16).astype(np.float32)
    return hi, lo


def host_consts():
    pos = np.arange(L, dtype=np.float32)
    qaug = np.zeros((NH, 4, L), np.float32)
    kaug = np.zeros((NH, 4, L), np.float32)
    for h in range(NH):
        slope = 2.0 ** (-8.0 * (h + 1) / NH)
        x = 8.0 * slope * pos
        hi, lo = _bf16_split(x)
        qaug[h, 0], qaug[h, 1], qaug[h, 2], qaug[h, 3] = -hi, -lo, 1.0, 1.0
        kaug[h, 0], kaug[h, 1], kaug[h, 2], kaug[h, 3] = 1.0, 1.0, hi, lo
    tri = (np.arange(128)[:, None] <= np.arange(128)[None, :]).astype(np.float32)
    return qaug, kaug, tri


def prep_shared(meta_tokens, norm_g, w_in, w_out, lambda_params, subln_g, conv_w, w_up, w_down):
    qaug, kaug, tri = host_consts()
    cst = np.zeros((128, 192), np.float32)
    cst[:, 0:128] = norm_g.reshape(16, NC8, 128).transpose(2, 0, 1).reshape(128, 128)
    cst[:, 128:176] = conv_w.reshape(2, 3, NC8, 128).transpose(3, 0, 1, 2).reshape(128, 48)
    cst[:, 176:178] = subln_g.T
    lamp = np.ascontiguousarray(np.broadcast_to(lambda_params.reshape(1, 512), (128, 512)))
    w_in_g = np.ascontiguousarray(
        w_in.reshape(DEPTH, D, 3, NC8, 128).transpose(0, 3, 1, 2, 4).reshape(DEPTH, NC8, D, 384))
    return {
        "metaT": np.ascontiguousarray(meta_tokens.T), "cst": cst, "lamp": lamp, "tri": tri, "qaug": qaug, "kaug": kaug,
        "w_in_g": w_in_g, "w_out": np.ascontiguousarray(w_out), "w_up": np.ascontiguousarray(w_up),
        "w_down": np.ascontiguousarray(w_down),
    }


_NC_CACHE = {}


def kernel(x, meta_tokens, norm_g, w_in, w_out, lambda_params, subln_g, conv_w, w_up, w_down):
    x = np.asarray(x, np.float32)
    args = [np.asarray(a, np.float32) for a in (meta_tokens, norm_g, w_in, w_out, lambda_params, subln_g, conv_w, w_up, w_down)]
    shared = prep_shared(*args)
    n_cores = 8
    per = x.shape[0] // n_cores
    if "nc" not in _NC_CACHE:
        _NC_CACHE["nc"] = build(n_seq=per)
    nc = _NC_CACHE["nc"]
    in_maps = []
    for c in range(n_cores):
        m = dict(shared)
        m["xT"] = np.ascontiguousarray(x[c * per:(c + 1) * per].transpose(0, 2, 1))
        in_maps.append(m)
    res = run_bass_kernel_spmd(nc, in_maps, core_ids=list(range(n_cores)))
    out = np.empty_like(x)
    for c in range(n_cores):
        out[c * per:(c + 1) * per] = res.results[c]["yT"].transpose(0, 2, 1)
    return out
```
